# Optimizing a Trainium2 kernel written in Bass

```python
import jax, jax.numpy as jnp
from jax import lax
import numpy as np

D_MODEL = 2048
BATCH = 4
SEQ = 2048
DEPTH = 4
DEC_BATCH = 8
DEC_SEQ = 8
PAST_LEN = 16384
PAGE_SIZE = 128

N_A_LAYERS = DEPTH // 2
N_B_LAYERS = DEPTH - N_A_LAYERS
D_FF = 4 * D_MODEL
CHUNK = 128
SGU_WIDTH = D_MODEL
SGU_GROUPS = 8
SGU_GROUP_DIM = SGU_WIDTH // SGU_GROUPS
N_HEADS = 16
HEAD_DIM = D_MODEL // N_HEADS
QUERY_BLOCK = 128
RMS_EPS = 1e-6
SB_BIAS_INIT = -6.0

kernel_name = "yoco_gmlp_stickbreaking_decoder_step"


def rmsnorm(x, g):
    xf = x.astype(jnp.float32)
    y = xf * lax.rsqrt(jnp.mean(xf * xf, axis=-1, keepdims=True) + RMS_EPS)
    return (y * g.astype(jnp.float32)).astype(x.dtype)


def sq_relu_mlp(x, w_up, w_down):
    h = jax.nn.relu(x @ w_up)
    return (h * h) @ w_down


def spatial_gating(z, w_s, b_s):
    B, T, E = z.shape
    L = min(T, CHUNK)
    n_chunks = T // L
    zc = z.reshape(B, n_chunks, L, SGU_GROUPS, SGU_GROUP_DIM)
    w = jnp.tril(w_s[:, :L, :L])
    s = jnp.einsum('gij,bnjgc->bnigc', w, zc) + b_s[:, :L].T[None, None, :, :, None]
    return s.reshape(B, T, E)


def chunk_mlp_mixer(h, w_in, g_v, w_s, b_s, w_out):
    u, v = jnp.split(h @ w_in, 2, axis=-1)
    z = rmsnorm(v, g_v)
    s = spatial_gating(z, w_s, b_s)
    return (u * s) @ w_out, z


def _stick_breaking_block(q, k, v, bias, q_pos, k_pos):
    z = jnp.einsum('bqhd,bkhd->bhqk', q, k).astype(jnp.float32) * (HEAD_DIM ** -0.5)
    z = z + bias.astype(jnp.float32)[None, :, None, None]
    mask = k_pos[None, :] < q_pos[:, None]
    log_beta = jax.nn.log_sigmoid(z)
    log_rem = jnp.where(mask, log_beta - z, 0.0)
    log_surv = lax.cumsum(log_rem, axis=3, reverse=True) - log_rem
    a = jnp.where(mask, jnp.exp(log_beta + log_surv), 0.0)
    return jnp.einsum('bhqk,bkhd->bqhd', a.astype(v.dtype), v)


def stick_breaking_attention(q, k, v, bias, q_pos, k_pos):
    B, Tq, H, Dh = q.shape
    if Tq <= QUERY_BLOCK:
        return _stick_breaking_block(q, k, v, bias, q_pos, k_pos)
    nb = Tq // QUERY_BLOCK
    qb = jnp.moveaxis(q.reshape(B, nb, QUERY_BLOCK, H, Dh), 1, 0)
    pb = q_pos.reshape(nb, QUERY_BLOCK)
    out = lax.map(lambda a: _stick_breaking_block(a[0], k, v, bias, a[1], k_pos), (qb, pb))
    return jnp.moveaxis(out, 0, 1).reshape(B, Tq, H, Dh)


def trunk(x, past_k, past_v, norm_mix, norm_ffn, w_ffn_up, w_ffn_down,
          a_w_in, a_norm_v, a_w_spatial, a_b_spatial, a_w_out,
          kv_norm, w_kv, b_w_q, b_logit_bias, b_w_out, norm_final):
    B, T, _ = x.shape
    past = 0 if past_k is None else past_k.shape[1]
    a_states = []
    for layer in range(DEPTH):
        h = rmsnorm(x, norm_mix[layer])
        if layer < N_A_LAYERS:
            i = layer
            mix, z = chunk_mlp_mixer(h, a_w_in[i], a_norm_v[i], a_w_spatial[i],
                                     a_b_spatial[i], a_w_out[i])
            a_states.append(z)
        else:
            if layer == N_A_LAYERS:
                k_h, v_h = jnp.split(rmsnorm(x, kv_norm) @ w_kv, 2, axis=-1)
                k_new = k_h.reshape(B, T, N_HEADS, HEAD_DIM)
                v_new = v_h.reshape(B, T, N_HEADS, HEAD_DIM)
                if past_k is None:
                    k_all, v_all = k_new, v_new
                else:
                    k_all = jnp.concatenate([past_k, k_new], axis=1)
                    v_all = jnp.concatenate([past_v, v_new], axis=1)
                q_pos = past + jnp.arange(T, dtype=jnp.int32)
                k_pos = jnp.arange(past + T, dtype=jnp.int32)
            j = layer - N_A_LAYERS
            q = (h @ b_w_q[j]).reshape(B, T, N_HEADS, HEAD_DIM)
            o = stick_breaking_attention(q, k_all, v_all, b_logit_bias[j], q_pos, k_pos)
            mix = o.reshape(B, T, N_HEADS * HEAD_DIM) @ b_w_out[j]
        x = x + mix
        x = x + sq_relu_mlp(rmsnorm(x, norm_ffn[layer]), w_ffn_up[layer], w_ffn_down[layer])
    return rmsnorm(x, norm_final), k_new, v_new, jnp.stack(a_states)


def setup_inputs(seed: int = 0) -> dict:
    key = jax.random.key(seed)
    ks = jax.random.split(key, 24)
    f32 = jnp.float32

    def w(k, shape, fan_in):
        return jax.random.normal(k, shape, f32) * (fan_in ** -0.5)

    def gain(k, shape):
        return 1.0 + 0.02 * jax.random.normal(k, shape, f32)

    n_pages = PAST_LEN // PAGE_SIZE
    n_used = DEC_BATCH * n_pages
    n_pool = n_used + max(1, n_used // 4)
    page_table = jax.random.permutation(ks[4], n_pool)[:n_used].reshape(DEC_BATCH, n_pages).astype(jnp.int32)
    hd = N_HEADS * HEAD_DIM
    return {
        "x_prompt": jax.random.normal(ks[0], (BATCH, SEQ, D_MODEL), f32),
        "x_sample": jax.random.normal(ks[1], (DEC_BATCH, DEC_SEQ, D_MODEL), f32),
        "cache_k": jax.random.normal(ks[2], (n_pool, PAGE_SIZE, N_HEADS, HEAD_DIM), f32),
        "cache_v": jax.random.normal(ks[3], (n_pool, PAGE_SIZE, N_HEADS, HEAD_DIM), f32),
        "page_table": page_table,
        "norm_mix": gain(ks[5], (DEPTH, D_MODEL)),
        "norm_ffn": gain(ks[6], (DEPTH, D_MODEL)),
        "w_ffn_up": w(ks[7], (DEPTH, D_MODEL, D_FF), D_MODEL),
        "w_ffn_down": w(ks[8], (DEPTH, D_FF, D_MODEL), D_FF),
        "a_w_in": w(ks[9], (N_A_LAYERS, D_MODEL, 2 * SGU_WIDTH), D_MODEL),
        "a_norm_v": gain(ks[10], (N_A_LAYERS, SGU_WIDTH)),
        "a_w_spatial": w(ks[11], (N_A_LAYERS, SGU_GROUPS, CHUNK, CHUNK), CHUNK),
        "a_b_spatial": gain(ks[12], (N_A_LAYERS, SGU_GROUPS, CHUNK)),
        "a_w_out": w(ks[13], (N_A_LAYERS, SGU_WIDTH, D_MODEL), SGU_WIDTH),
        "kv_norm": gain(ks[14], (D_MODEL,)),
        "w_kv": w(ks[15], (D_MODEL, 2 * hd), D_MODEL),
        "b_w_q": w(ks[16], (N_B_LAYERS, D_MODEL, hd), D_MODEL),
        "b_logit_bias": SB_BIAS_INIT + 0.1 * jax.random.normal(ks[19], (N_B_LAYERS, N_HEADS), f32),
        "b_w_out": w(ks[17], (N_B_LAYERS, hd, D_MODEL), hd),
        "norm_final": gain(ks[18], (D_MODEL,)),
    }


def reference(x_prompt, x_sample, cache_k, cache_v, page_table,
              norm_mix, norm_ffn, w_ffn_up, w_ffn_down,
              a_w_in, a_norm_v, a_w_spatial, a_b_spatial, a_w_out,
              kv_norm, w_kv, b_w_q, b_logit_bias, b_w_out, norm_final):
    y_prompt, k_prompt, v_prompt, _ = trunk(
        x_prompt, None, None, norm_mix, norm_ffn, w_ffn_up, w_ffn_down,
        a_w_in, a_norm_v, a_w_spatial, a_b_spatial, a_w_out,
        kv_norm, w_kv, b_w_q, b_logit_bias, b_w_out, norm_final)
    db = x_sample.shape[0]
    past_len = page_table.shape[1] * cache_k.shape[1]
    past_k = cache_k[page_table].reshape(db, past_len, N_HEADS, HEAD_DIM)
    past_v = cache_v[page_table].reshape(db, past_len, N_HEADS, HEAD_DIM)
    y_sample, k_sample, v_sample, sgu_v_sample = trunk(
        x_sample, past_k, past_v, norm_mix, norm_ffn, w_ffn_up, w_ffn_down,
        a_w_in, a_norm_v, a_w_spatial, a_b_spatial, a_w_out,
        kv_norm, w_kv, b_w_q, b_logit_bias, b_w_out, norm_final)
    return (y_prompt, y_sample, k_prompt, v_prompt, k_sample, v_sample, sgu_v_sample)
```

```python
import numpy as np
import ml_dtypes
from contextlib import ExitStack
import concourse.bass as bass
import concourse.mybir as mybir
from concourse.bass_utils import run_bass_kernel_spmd

F32 = mybir.dt.float32
BF16 = mybir.dt.bfloat16
I32 = mybir.dt.int32
AF = mybir.ActivationFunctionType
ALU = mybir.AluOpType

D = 2048
NCH = 16
TP = 1024
TS = 8
TT = TP + TS
DFF = 8192
NH = 16
DEPTH = 4
NPAGES = 128
NPOOL = 1280
GROUPS = [(0, 512), (512, 512), (1024, 8)]
EPS = 1e-6
SCALE = 128.0 ** -0.5
ENG = ["pe", "act", "dve", "pool", "sp"]
DBG_SUB = 0


class Res:
    __slots__ = ("w", "r", "const", "excl")

    def __init__(self, const=False, excl=False):
        self.w = None
        self.r = []
        self.const = const
        self.excl = excl


class Ins:
    __slots__ = ("eng", "fn", "deps", "sig", "sigval", "kind", "sem", "val", "pos")


class Prog:
    def __init__(self, nc, ndma=32):
        self.nc = nc
        self.streams = {e: [] for e in ENG}
        self.ndma = ndma
        self.dma_hist = {"sp": [], "pool": []}
        self.cc_n = 0
        self.out_dmas = []

    def _mk(self, eng, fn, reads, writes, kind):
        ins = Ins()
        ins.eng = eng
        ins.fn = fn
        ins.kind = kind
        ins.sig = False
        ins.sigval = 0
        ins.sem = 0
        ins.val = 0
        ins.pos = len(self.streams[eng])
        deps = {}

        def add(d):
            if d is None:
                return
            if d.kind == "op":
                if d.eng == "pe" and eng == "pe" and kind == "op":
                    return
                k = ("e", d.eng)
                if k not in deps or deps[k].pos < d.pos:
                    deps[k] = d
            else:
                deps[("x", id(d))] = d

        for r in reads:
            add(r.w)
            if r.excl:
                for x in r.r:
                    if x.eng != eng:
                        add(x)
        for w in writes:
            add(w.w)
            for x in w.r:
                add(x)
        if kind == "dma":
            hist = self.dma_hist[eng]
            n = len(hist)
            ins.sem = (eng, n % self.ndma)
            ins.val = 16 * (n // self.ndma + 1)
            if n >= self.ndma:
                add(hist[n - self.ndma])
            hist.append(ins)
        elif kind == "cc":
            self.cc_n += 1
            ins.val = self.cc_n
        ins.deps = list(deps.values())
        for r in reads:
            if not r.const:
                r.r.append(ins)
        for w in writes:
            w.w = ins
            w.r = []
        self.streams[eng].append(ins)
        return ins

    def op(self, eng, fn, reads=(), writes=()):
        return self._mk(eng, fn, reads, writes, "op")

    def dma(self, eng, fn, reads=(), writes=(), is_out=False):
        i = self._mk(eng, fn, reads, writes, "dma")
        if is_out:
            self.out_dmas.append(i)
        return i

    def cc(self, fn, reads=(), writes=()):
        return self._mk("pool", fn, reads, writes, "cc")

    def emit(self):
        nc = self.nc
        fin = Ins()
        fin.eng = "sp"
        fin.fn = None
        fin.kind = "op"
        fin.sig = False
        fin.sigval = 0
        fin.pos = len(self.streams["sp"])
        fin.deps = list(self.out_dmas)
        self.streams["sp"].append(fin)
        for e in ENG:
            for ins in self.streams[e]:
                for d in ins.deps:
                    if d.kind == "op":
                        d.sig = True
        for e in ENG:
            cnt = 0
            for ins in self.streams[e]:
                if ins.kind == "op" and ins.sig:
                    cnt += 1
                    ins.sigval = cnt
        with ExitStack() as st:
            esem = {e: st.enter_context(nc.semaphore("s_" + e)) for e in ENG}
            dsem = {(q, i): st.enter_context(nc.semaphore("d%s%d" % (q, i)))
                    for q in ("sp", "pool") for i in range(self.ndma)}
            ccsem = st.enter_context(nc.semaphore("ccs"))
            block = st.enter_context(nc.Block())

            def run(e, eng):
                waited = {}
                for ins in self.streams[e]:
                    for d in ins.deps:
                        if d.kind == "op":
                            s, v, key = esem[d.eng], d.sigval, ("e", d.eng)
                        elif d.kind == "dma":
                            s, v, key = dsem[d.sem], d.val, ("d", d.sem)
                        else:
                            s, v, key = ccsem, d.val, ("c",)
                        if waited.get(key, 0) >= v:
                            continue
                        waited[key] = v
                        eng.wait_ge(s, v)
                    if ins.fn is None:
                        continue
                    r = ins.fn(eng)
                    if ins.kind == "op":
                        if ins.sig:
                            r.then_inc(esem[e], 1)
                    elif ins.kind == "dma":
                        r.then_inc(dsem[ins.sem], 16)
                    else:
                        r.then_inc(ccsem)

            block.tensor(lambda eng: run("pe", eng))
            block.scalar(lambda eng: run("act", eng))
            block.vector(lambda eng: run("dve", eng))
            block.gpsimd(lambda eng: run("pool", eng))
            block.sync(lambda eng: run("sp", eng))


def build_program(stage=9, npool=NPOOL):
    nc = bass.Bass("TRN2", target_bir_lowering=False)

    def din(name, shape, dt=F32):
        return nc.dram_tensor(name, list(shape), dt, kind="ExternalInput").ap()

    def dout(name, shape, dt=F32):
        return nc.dram_tensor(name, list(shape), dt, kind="ExternalOutput").ap()

    xp = din("xp", [TP, D])
    xs = din("xs", [TS, D])
    cache_k = din("cache_k", [npool * 128, D])
    cache_v = din("cache_v", [npool * 128, D])
    ptab = din("ptab", [1, NPAGES], I32)
    gains_d = din("gains", [9, D])
    gv_d = din("gv", [2, D])
    gfin_d = din("gfin", [1, D])
    w_up = din("w_ffn_up", [DEPTH, D, DFF])
    w_down = din("w_ffn_down", [DEPTH, DFF, D])
    a_w_in = din("a_w_in", [2, D, 2 * D])
    a_w_sp = din("a_w_spatial", [2, 8, 128, 128])
    a_b_sp = din("a_b_spatial", [2, 8, 128])
    a_w_out = din("a_w_out", [2, D, D])
    w_kv = din("w_kv", [D, 2 * D])
    b_w_q = din("b_w_q", [2, D, D])
    b_w_out = din("b_w_out", [2, D, D])
    cst_f = din("cst_f", [128, 192])
    cst_b = din("cst_b", [128, 1160], BF16)

    yp = dout("yp", [TP, D])
    ys = dout("ys", [TS, D])
    kp = dout("kp", [TP, D])
    vp = dout("vp", [TP, D])
    ks = dout("ks", [TS, D])
    vs = dout("vs", [TS, D])
    zs = dout("zs", [2, TS, D])

    kt_b = [nc.dram_tensor("kt_b%d" % i, [D // 2, TP], BF16) for i in range(2)]
    kt_g = [nc.dram_tensor("kt_g%d" % i, [D, TP], BF16) for i in range(2)]
    v_b = [nc.dram_tensor("v_b%d" % i, [TP // 2, D], BF16) for i in range(2)]
    v_g = [nc.dram_tensor("v_g%d" % i, [TP, D], BF16) for i in range(2)]
    vs_bd = nc.dram_tensor("vs_bd", [TS, D], BF16)

    P = Prog(nc)
    st = ExitStack()

    def sb(name, shape, dt):
        return st.enter_context(nc.sbuf_tensor(name, list(shape), dt))

    XT = sb("XT", [128, NCH, TT], F32)
    HB = sb("HB", [128, NCH, TT], BF16)
    QB = sb("QB", [128, NCH, TT], BF16)
    NW = 3
    WR = sb("WR", [128, NW, 4096], BF16)
    ARN = 9472
    AR = sb("AR", [128, ARN], F32)
    ARB = AR[:, :].bitcast(BF16)
    CF = sb("CF", [128, 192], F32)
    CB = sb("CB", [128, 1160], BF16)
    GN = sb("GN", [128, 11, NCH], F32)
    SM = sb("SM", [128, 64], F32)
    T0 = sb("T0", [128, 2, 512], F32)
    T1 = sb("T1", [128, 2, 512], BF16)
    IDX = sb("IDX", [128, NPAGES], I32)
    IOT = sb("IOT", [128, 2], F32)
    IOTI = sb("IOTI", [128, 2], I32)
    KTS = sb("KTS", [128, NH, TS], BF16)
    A2 = sb("A2", [128, 1472], F32)

    PS = st.enter_context(nc.psum_tensor("PS", [128, 6, 512], F32))
    PB = st.enter_context(nc.psum_tensor("PB", [128, 2, 1024], BF16))

    r_XT = [[Res() for _ in GROUPS] for _ in range(NCH)]
    r_HB = [[Res() for _ in GROUPS] for _ in range(NCH)]
    r_QB = [[Res() for _ in GROUPS] for _ in range(NCH)]
    r_QB_all = [r for row in r_QB for r in row]
    r_WR = [Res() for _ in range(NW)]
    r_PS = [Res(excl=True) for _ in range(6)]
    r_PB = [Res(excl=True) for _ in range(2)]
    r_CF = Res(const=True)
    r_CB = Res(const=True)
    r_GN = Res(const=True)
    r_T0 = [Res(), Res()]
    r_T1 = [Res(), Res()]
    r_IDX = Res(const=True)
    r_KTS = Res()
    r_vs_d = Res()

    ident_f = CF[:, 0:128]
    blb_hq = CF[:, 128:130]
    blb_bc = CF[:, 130:162]
    ident_b = CB[:, 0:128]
    ones_b = CB[:, 128:640]
    tril_b = CB[:, 640:768]
    mask2 = CB[:, 768:1024]
    masknew = CB[:, 1024:1032]
    zeros_b = CB[:, 1032:1160]
    EPSB = SM[:, 0:1]

    arena = {"users": []}

    def arena_phase(new):
        for nr in new:
            for o in arena["users"]:
                if o.w is not None:
                    nr.r.append(o.w)
                nr.r.extend(o.r)
        arena["users"] = list(new)

    cnt = {"ps": 0, "pb": 0, "wr": 0, "t0": 0, "t1": 0}

    def next_ps():
        i = cnt["ps"] % 4
        cnt["ps"] += 1
        return PS[:, i, :], r_PS[i]

    def next_pb():
        i = cnt["pb"] % 2
        cnt["pb"] += 1
        return PB[:, i, 0:512], r_PB[i]

    def next_t0():
        i = cnt["t0"] % 2
        cnt["t0"] += 1
        return T0[:, i, :], r_T0[i]

    def next_t1():
        i = cnt["t1"] % 2
        cnt["t1"] += 1
        return T1[:, i, :], r_T1[i]

    def load_w(src2d, r0, nk, c0):
        ncols = 4096 // nk
        i = cnt["wr"] % NW
        cnt["wr"] += 1
        dst = WR[:, i, :].rearrange("p (k c) -> p k c", k=nk)
        src = src2d[r0:r0 + nk * 128, c0:c0 + ncols].rearrange("(k p) c -> p k c", p=128)
        P.dma("pool", lambda e: e.dma_start(out=dst, in_=src), writes=[r_WR[i]])
        return dst, r_WR[i]

    P.dma("sp", lambda e: e.dma_start(out=CF[:, :], in_=cst_f[:, :]), writes=[r_CF])
    P.dma("sp", lambda e: e.dma_start(out=CB[:, :], in_=cst_b[:, :]), writes=[r_CB])
    with nc.allow_non_contiguous_dma(reason="small gain vectors, feature-major"):
        for gi in range(9):
            P.dma("sp", lambda e, gi=gi: e.dma_start(
                out=GN[:, gi, :], in_=gains_d[gi, :].rearrange("(c p) -> p c", p=128),
                allow_slow_non_contiguous=True), writes=[r_GN])
        for gi in range(2):
            P.dma("sp", lambda e, gi=gi: e.dma_start(
                out=GN[:, 9 + gi, :], in_=gv_d[gi, :].rearrange("(c p) -> p c", p=128),
                allow_slow_non_contiguous=True), writes=[r_GN])
    P.dma("sp", lambda e: e.dma_start(out=IDX[:, :], in_=ptab[0:1, :].partition_broadcast(128)), writes=[r_IDX])
    P.op("pool", lambda e: e.iota(IOTI[:, 0:1], [[0, 1]], base=0, channel_multiplier=1), writes=[r_IDX])
    P.op("dve", lambda e: e.tensor_copy(out=IOT[:, 0:1], in_=IOTI[:, 0:1]), reads=[r_IDX], writes=[r_IDX])
    P.op("dve", lambda e: e.tensor_scalar(out=IDX[:, :], in0=IDX[:, :], scalar1=128.0, scalar2=IOT[:, 0:1],
                                          op0=ALU.mult, op1=ALU.add), reads=[r_IDX], writes=[r_IDX])
    P.op("dve", lambda e: e.memset(SM[:, 0:1], EPS), writes=[r_GN])

    XIN = AR[:, 0:2048]
    r_XIN = Res()
    arena_phase([r_XIN])
    for c in range(9):
        rows = 128 if c < 8 else 8
        src = xp[c * 128:(c + 1) * 128, :] if c < 8 else xs[:, :]
        g = c // 4 if c < 8 else 2
        P.dma("sp", lambda e, src=src, rows=rows: e.dma_start(out=XIN[0:rows, :], in_=src), writes=[r_XIN])
        for q4 in range(4):
            ps, rps = next_ps()
            for j in range(4):
                ch = q4 * 4 + j
                P.op("pe", lambda e, ps=ps, j=j, ch=ch, rows=rows: e.transpose(
                    out=ps[:, j * 128:j * 128 + rows], in_=XIN[0:rows, ch * 128:(ch + 1) * 128],
                    identity=ident_f[0:rows, 0:rows]), reads=[r_XIN, r_CF], writes=[rps])
            col = c * 128
            dst = XT[:, q4 * 4:q4 * 4 + 4, col:col + rows]
            srcp = ps.rearrange("p (j t) -> p j t", j=4)[:, :, 0:rows]
            wr = [r_XT[q4 * 4 + j][g] for j in range(4)]
            if q4 % 2 == 0:
                P.op("act", lambda e, dst=dst, srcp=srcp: e.copy(out=dst, in_=srcp), reads=[rps], writes=wr)
            else:
                P.op("dve", lambda e, dst=dst, srcp=srcp: e.tensor_copy(out=dst, in_=srcp), reads=[rps], writes=wr)

    def rmsnorm_fm(gi, DST, r_DST):
        for g, (g0, n) in enumerate(GROUPS):
            ps, rps = next_ps()
            for c in range(NCH):
                t1, rt1 = next_t1()
                P.op("act", lambda e, t1=t1, c=c, g0=g0, n=n: e.activation(
                    out=t1[:, 0:n], in_=XT[:, c, g0:g0 + n], func=AF.Square), reads=[r_XT[c][g]], writes=[rt1])
                P.op("pe", lambda e, ps=ps, t1=t1, c=c, n=n: e.matmul(
                    ps[:, 0:n], lhsT=ones_b[:, 0:128], rhs=t1[:, 0:n], start=(c == 0), stop=(c == NCH - 1)),
                    reads=[rt1, r_CB], writes=[rps])
            t0, rt0 = next_t0()
            P.op("act", lambda e, t0=t0, ps=ps, n=n: e.activation(
                out=t0[:, 0:n], in_=ps[:, 0:n], func=AF.Sqrt, bias=EPSB, scale=1.0 / D),
                reads=[rps, r_GN], writes=[rt0])
            P.op("dve", lambda e, t0=t0, n=n: e.reciprocal(out=t0[:, 0:n], in_=t0[:, 0:n]), reads=[rt0], writes=[rt0])
            for c in range(NCH):
                P.op("dve", lambda e, t0=t0, c=c, g0=g0, n=n: e.scalar_tensor_tensor(
                    out=DST[:, c, g0:g0 + n], in0=XT[:, c, g0:g0 + n], scalar=GN[:, gi, c:c + 1], in1=t0[:, 0:n],
                    op0=ALU.mult, op1=ALU.mult), reads=[r_XT[c][g], rt0, r_GN], writes=[r_DST[c][g]])

    def linear_fm(Wd, r0, nk, c_base, ncols, SRC, r_SRC, consumer):
        tcols = 4096 // nk
        for cb in range(ncols // tcols):
            wt, rwt = load_w(Wd, r0, nk, c_base + cb * tcols)
            for mm in range(tcols // 128):
                m = cb * (tcols // 128) + mm
                for g, (g0, n) in enumerate(GROUPS):
                    ps, rps = next_ps()
                    for k in range(nk):
                        P.op("pe", lambda e, ps=ps, wt=wt, k=k, mm=mm, g0=g0, n=n: e.matmul(
                            ps[:, 0:n], lhsT=wt[:, k, mm * 128:(mm + 1) * 128], rhs=SRC[:, k, g0:g0 + n],
                            start=(k == 0), stop=(k == nk - 1)), reads=[rwt, r_SRC[k][g]], writes=[rps])
                    consumer(m, g, g0, n, ps, rps)

    def add_to_x(m, g, g0, n, ps, rps):
        P.op("dve", lambda e: e.tensor_tensor(out=XT[:, m, g0:g0 + n], in0=ps[:, 0:n], in1=XT[:, m, g0:g0 + n],
                                              op=ALU.add), reads=[rps, r_XT[m][g]], writes=[r_XT[m][g]])

    def ffn(layer):
        rmsnorm_fm(4 + layer, QB, r_QB)
        for s in range(8):
            slot = s % 2
            r_H = [r_HB[slot * 8 + k] for k in range(8)]

            def up_cons(m, g, g0, n, ps, rps, slot=slot, r_H=r_H):
                t0, rt0 = next_t0()
                P.op("act", lambda e: e.activation(out=t0[:, 0:n], in_=ps[:, 0:n], func=AF.Relu),
                     reads=[rps], writes=[rt0])
                P.op("dve", lambda e: e.tensor_tensor(out=HB[:, slot * 8 + m, g0:g0 + n], in0=t0[:, 0:n],
                                                      in1=t0[:, 0:n], op=ALU.mult),
                     reads=[rt0], writes=[r_H[m][g]])

            linear_fm(w_up[layer], 0, 16, s * 1024, 1024, QB, r_QB, up_cons)
            linear_fm(w_down[layer], s * 1024, 8, 0, D, HB[:, slot * 8:slot * 8 + 8, :], r_H, add_to_x)

    def linear_tm(Wd, c_base, ncols, SRC, r_SRC, consumer):
        for cb in range(ncols // 256):
            wt, rwt = load_w(Wd, 0, 16, c_base + cb * 256)
            for tc in range(9):
                rows = 128 if tc < 8 else 8
                g = tc // 4 if tc < 8 else 2
                ps, rps = next_ps()
                for k in range(NCH):
                    P.op("pe", lambda e, ps=ps, wt=wt, k=k, tc=tc, rows=rows: e.matmul(
                        ps[0:rows, 0:256], lhsT=SRC[:, k, tc * 128:tc * 128 + rows], rhs=wt[:, k, :],
                        start=(k == 0), stop=(k == NCH - 1)), reads=[rwt, r_SRC[k][g]], writes=[rps])
                consumer(cb, tc, rows, ps, rps)

    def a_mixer(l):
        rmsnorm_fm(l, HB, r_HB)
        if DBG_SUB == 1:
            return
        Wi = a_w_in[l]
        VT = ARB[:, 0:18432].rearrange("p (c e) -> p c e", c=9)
        r_VT = [Res() for _ in range(9)]
        arena_phase(r_VT)
        SSP = SM[:, 8:8 + 72].rearrange("p (c b) -> p c b", c=9) if False else None
        SSP = A2[:, 1392:1464].rearrange("p (c b) -> p c b", c=9)
        RST = SM[:, 8:17]
        r_SS = Res()
        P.op("dve", lambda e: e.memset(SSP, 0.0), writes=[r_SS])
        QBF = QB[:, :, :].rearrange("p c t -> p (c t)").bitcast(F32)
        VSF = QBF[0:8, 0:2048]
        GVB = QBF[0:8, 2048:4096]
        P.dma("sp", lambda e: e.dma_start(out=GVB, in_=gv_d[l:l + 1, :].partition_broadcast(8)),
              writes=r_QB_all)

        def v_cons(cb, tc, rows, ps, rps):
            t1, rt1 = next_t1()
            P.op("act", lambda e: e.activation(out=t1[0:rows, 0:256], in_=ps[0:rows, 0:256], func=AF.Square,
                                               accum_out=SSP[0:rows, tc, cb:cb + 1]),
                 reads=[rps], writes=[rt1, r_SS])
            P.op("dve", lambda e: e.tensor_copy(out=VT[0:rows, tc, cb * 256:(cb + 1) * 256], in_=ps[0:rows, 0:256]),
                 reads=[rps], writes=[r_VT[tc]])
            if tc == 8:
                P.op("act", lambda e: e.copy(out=VSF[:, cb * 256:(cb + 1) * 256], in_=ps[0:8, 0:256]),
                     reads=[rps], writes=r_QB_all)

        linear_tm(Wi, D, D, HB, r_HB, v_cons)
        P.op("dve", lambda e: e.tensor_reduce(out=RST, in_=SSP, axis=mybir.AxisListType.X, op=ALU.add),
             reads=[r_SS], writes=[r_SS])
        P.op("act", lambda e: e.activation(out=RST, in_=RST, func=AF.Sqrt, bias=EPSB, scale=1.0 / D),
             reads=[r_SS, r_GN], writes=[r_SS])
        P.op("dve", lambda e: e.reciprocal(out=RST, in_=RST), reads=[r_SS], writes=[r_SS])
        P.op("dve", lambda e: e.scalar_tensor_tensor(out=VSF, in0=VSF, scalar=RST[0:8, 8:9], in1=GVB,
                                                     op0=ALU.mult, op1=ALU.mult),
             reads=[r_SS] + r_QB_all, writes=r_QB_all)
        P.dma("sp", lambda e: e.dma_start(out=zs[l, :, :], in_=VSF), reads=r_QB_all, is_out=True)
        if DBG_SUB == 2:
            return
        WSPg = [A2[:, s * 128:(s + 1) * 128] for s in range(2)]
        WTMg = [A2[:, 256 + s * 128:256 + (s + 1) * 128] for s in range(2)]
        BSBg = [A2[:, 512 + s * 128:512 + (s + 1) * 128] for s in range(2)]
        WCg = A2[:, 768:1344].bitcast(BF16).rearrange("p (c i) -> p c i", c=9)
        r_WSP = [Res(), Res()]
        r_WTM = [Res(), Res()]
        r_BSB = [Res(), Res()]
        r_WC = Res()
        for cb in range(8):
            wt, rwt = load_w(Wi, 0, 16, cb * 256)
            gsp = cb
            s2 = gsp % 2
            P.dma("sp", lambda e, s2=s2, gsp=gsp: e.dma_start(out=WSPg[s2], in_=a_w_sp[l, gsp, :, :]),
                  writes=[r_WSP[s2]])
            P.dma("sp", lambda e, s2=s2, gsp=gsp: e.dma_start(
                out=BSBg[s2], in_=a_b_sp[l, gsp:gsp + 1, :].partition_broadcast(128)), writes=[r_BSB[s2]])
            ps, rps = next_ps()
            P.op("pe", lambda e, ps=ps, s2=s2: e.transpose(out=ps[:, 0:128], in_=WSPg[s2], identity=ident_f),
                 reads=[r_WSP[s2], r_CF], writes=[rps])
            P.op("dve", lambda e, ps=ps, s2=s2: e.tensor_tensor(out=WTMg[s2], in0=ps[:, 0:128], in1=tril_b,
                                                                op=ALU.mult),
                 reads=[rps, r_CB], writes=[r_WTM[s2]])
            for tc in range(9):
                rows = 128 if tc < 8 else 8
                P.op("dve", lambda e, tc=tc, rows=rows, s2=s2: e.tensor_scalar(
                    out=WCg[0:rows, tc, :], in0=WTMg[s2][0:rows, :], scalar1=RST[0:rows, tc:tc + 1], scalar2=None,
                    op0=ALU.mult), reads=[r_WTM[s2], r_SS], writes=[r_WC])
            for mm in range(2):
                m = cb * 2 + mm
                for g, (g0, n) in enumerate(GROUPS):
                    pu, rpu = next_ps()
                    for k in range(NCH):
                        P.op("pe", lambda e, pu=pu, wt=wt, k=k, mm=mm, g0=g0, n=n: e.matmul(
                            pu[:, 0:n], lhsT=wt[:, k, mm * 128:(mm + 1) * 128], rhs=HB[:, k, g0:g0 + n],
                            start=(k == 0), stop=(k == NCH - 1)), reads=[rwt, r_HB[k][g]], writes=[rpu])
                    pss, rpss = next_ps()
                    t0, rt0 = next_t0()
                    if g < 2:
                        for j in range(4):
                            tc = g * 4 + j
                            P.op("pe", lambda e, pss=pss, j=j, tc=tc, m=m: e.matmul(
                                pss[:, j * 128:(j + 1) * 128], lhsT=VT[:, tc, m * 128:(m + 1) * 128],
                                rhs=WCg[:, tc, :], start=True, stop=True), reads=[r_VT[tc], r_WC], writes=[rpss])
                        for j in range(4):
                            P.op("dve", lambda e, pss=pss, t0=t0, j=j, m=m, s2=s2: e.scalar_tensor_tensor(
                                out=t0[:, j * 128:(j + 1) * 128], in0=pss[:, j * 128:(j + 1) * 128],
                                scalar=GN[:, 9 + l, m:m + 1], in1=BSBg[s2], op0=ALU.mult, op1=ALU.add),
                                reads=[rpss, r_BSB[s2], r_GN], writes=[rt0])
                    else:
                        P.op("pe", lambda e, pss=pss, m=m: e.matmul(
                            pss[:, 0:8], lhsT=VT[0:8, 8, m * 128:(m + 1) * 128], rhs=WCg[0:8, 8, 0:8],
                            start=True, stop=True), reads=[r_VT[8], r_WC], writes=[rpss])
                        P.op("dve", lambda e, pss=pss, t0=t0, m=m, s2=s2: e.scalar_tensor_tensor(
                            out=t0[:, 0:8], in0=pss[:, 0:8], scalar=GN[:, 9 + l, m:m + 1], in1=BSBg[s2][:, 0:8],
                            op0=ALU.mult, op1=ALU.add), reads=[rpss, r_BSB[s2], r_GN], writes=[rt0])
                    P.op("dve", lambda e, pu=pu, t0=t0, m=m, g0=g0, n=n: e.tensor_tensor(
                        out=QB[:, m, g0:g0 + n], in0=pu[:, 0:n], in1=t0[:, 0:n], op=ALU.mult),
                        reads=[rpu, rt0], writes=[r_QB[m][g]])
        if DBG_SUB == 3:
            return
        linear_fm(a_w_out[l], 0, 16, 0, D, QB, r_QB, add_to_x)

    for l in range(2):
        if stage >= 1 + l:
            a_mixer(l)
            if not DBG_SUB:
                ffn(l)

    r_ktg = [Res(), Res()]
    r_vg = [Res(), Res()]
    r_vs_l = []
    def kv_phase():
        rmsnorm_fm(8, HB, r_HB)
        KVO = AR[:, 0:512].rearrange("p (s c) -> p s c", s=2)
        r_KVO = [Res(), Res()]
        VBS = ARB[:, 1024:1536].rearrange("p (s c) -> p s c", s=2)
        r_VBS = [Res(), Res()]
        VSSt = ARB[:, 1536:3584]
        KTS_st = ARB[:, 3584:4608].rearrange("p (s c) -> p s c", s=2)
        r_KTS_st = [Res(), Res()]
        arena_phase(r_KVO + r_VBS + r_KTS_st)
        r_vb = [[], []]
        r_ktb = [[], []]
        kvc = {"o": 0, "b": 0, "k": 0}

        def kv_cons(cb, tc, rows, ps, rps):
            i = kvc["o"] % 2
            kvc["o"] += 1
            P.op("act", lambda e: e.copy(out=KVO[0:rows, i, :], in_=ps[0:rows, 0:256]), reads=[rps], writes=[r_KVO[i]])
            isk = cb < 8
            col = (cb % 8) * 256
            if tc < 8:
                dst = (kp if isk else vp)[tc * 128:(tc + 1) * 128, col:col + 256]
                P.dma("sp", lambda e: e.dma_start(out=dst, in_=KVO[0:rows, i, :]), reads=[r_KVO[i]], is_out=True)
            else:
                dst = (ks if isk else vs)[:, col:col + 256]
                P.dma("sp", lambda e: e.dma_start(out=dst, in_=KVO[0:rows, i, :]), reads=[r_KVO[i]], is_out=True)
                if not isk:
                    b = kvc["b"] % 2
                    kvc["b"] += 1
                    P.op("dve", lambda e: e.tensor_copy(out=VBS[0:8, b, :], in_=ps[0:8, 0:256]), reads=[rps],
                         writes=[r_VBS[b]])
                    r_vs_l.append(Res())
                    P.dma("sp", lambda e: e.dma_start(out=vs_bd.ap()[:, col:col + 256], in_=VBS[0:8, b, :]),
                          reads=[r_VBS[b]], writes=[r_vs_l[-1]])
            if (not isk) and tc < 8:
                b = kvc["b"] % 2
                kvc["b"] += 1
                P.op("dve", lambda e: e.tensor_copy(out=VBS[:, b, :], in_=ps[:, 0:256]), reads=[rps], writes=[r_VBS[b]])
                r_vb[tc // 4].append(Res())
                P.dma("sp", lambda e: e.dma_start(out=v_b[tc // 4].ap()[(tc % 4) * 128:(tc % 4 + 1) * 128, col:col + 256],
                                                  in_=VBS[:, b, :]), reads=[r_VBS[b]], writes=[r_vb[tc // 4][-1]])

        linear_tm(w_kv, 0, 2 * D, HB, r_HB, kv_cons)

        def kt_cons(m, g, g0, n, ps, rps):
            if g < 2:
                b = kvc["k"] % 2
                kvc["k"] += 1
                P.op("act", lambda e: e.copy(out=KTS_st[:, b, :], in_=ps[:, :]), reads=[rps], writes=[r_KTS_st[b]])
                r_ktb[m // 8].append(Res())
                P.dma("sp", lambda e: e.dma_start(out=kt_b[m // 8].ap()[(m % 8) * 128:(m % 8 + 1) * 128, g0:g0 + 512],
                                                  in_=KTS_st[:, b, :]), reads=[r_KTS_st[b]], writes=[r_ktb[m // 8][-1]])
            else:
                P.op("act", lambda e: e.copy(out=KTS[:, m, :], in_=ps[:, 0:8]), reads=[rps], writes=[r_KTS])

        linear_fm(w_kv, 0, 16, 0, D, HB, r_HB, kt_cons)

        RG = [[0, 1], [2, 3], [4, 5], [6, 7]]
        for hf in range(2):
            P.cc(lambda e, hf=hf: e.collective_compute("AllGather", ALU.bypass, replica_groups=RG,
                                                       ins=[kt_b[hf].ap().opt()], outs=[kt_g[hf].ap().opt()]),
                 reads=r_ktb[hf], writes=[r_ktg[hf]])
        for hf in range(2):
            P.cc(lambda e, hf=hf: e.collective_compute("AllGather", ALU.bypass, replica_groups=RG,
                                                       ins=[v_b[hf].ap().opt()], outs=[v_g[hf].ap().opt()]),
                 reads=r_vb[hf], writes=[r_vg[hf]])


    if stage >= 3:
        kv_phase()

    NEGC = SM[:, 24:56]
    r_NC = [Res() for _ in range(32)]
    scn = {"seg": 0, "nc": 0}

    def sb_segment(bufs, n, sc_ps, r_sc, bias_ap, mask_ap, mask_cols, first, prev_nc):
        s = scn["seg"] % 2
        scn["seg"] += 1
        ci = scn["nc"] % 32
        scn["nc"] += 1
        E, L, Pq, A = bufs["E"][s], bufs["L"][s], bufs["P"][s], bufs["A"][s]
        rE, rL, rP, rA = bufs["rE"][s], bufs["rL"][s], bufs["rP"][s], bufs["rA"][s]
        P.op("act", lambda e: e.activation(out=E[:, 0:n], in_=sc_ps[:, 0:n], func=AF.Exp, bias=bias_ap,
                                           scale=SCALE), reads=[r_sc, r_CF], writes=[rE])
        if mask_ap is not None:
            c0 = n - mask_cols
            P.op("dve", lambda e: e.tensor_tensor(out=E[:, c0:n], in0=E[:, c0:n], in1=mask_ap, op=ALU.mult),
                 reads=[rE, r_CB], writes=[rE])
        P.op("act", lambda e: e.activation(out=L[:, 0:n], in_=E[:, 0:n], func=AF.Ln, bias=1.0, scale=1.0),
             reads=[rE], writes=[rL])
        P.op("dve", lambda e: e.tensor_tensor_scan(out=Pq[:, 1:n + 1], data0=ones_b[:, 0:n], data1=L[:, 0:n],
                                                   initial=0.0, op0=ALU.mult, op1=ALU.add),
             reads=[rL, r_CB], writes=[rP])
        if first:
            P.op("dve", lambda e: e.tensor_scalar(out=NEGC[:, ci:ci + 1], in0=Pq[:, n:n + 1], scalar1=-1.0,
                                                  scalar2=None, op0=ALU.mult), reads=[rP], writes=[r_NC[ci]])
        else:
            P.op("dve", lambda e: e.tensor_tensor(out=NEGC[:, ci:ci + 1], in0=NEGC[:, prev_nc:prev_nc + 1],
                                                  in1=Pq[:, n:n + 1], op=ALU.subtract),
                 reads=[rP, r_NC[prev_nc]], writes=[r_NC[ci]])
        P.op("act", lambda e: e.activation(out=L[:, 0:n], in_=Pq[:, 0:n], func=AF.Exp, bias=NEGC[:, ci:ci + 1],
                                           scale=1.0), reads=[rP, r_NC[ci]], writes=[rL])
        P.op("dve", lambda e: e.tensor_tensor(out=A[:, 0:n], in0=E[:, 0:n], in1=L[:, 0:n], op=ALU.mult),
             reads=[rE, rL], writes=[rA])
        return s, ci

    def b_mixer(j, do_sample=True):
        layer = 2 + j
        rmsnorm_fm(layer, HB, r_HB)

        def q_cons(m, g, g0, n, ps, rps):
            P.op("act", lambda e: e.copy(out=QB[:, m, g0:g0 + n], in_=ps[:, 0:n]), reads=[rps], writes=[r_QB[m][g]])

        linear_fm(b_w_q[j], 0, 16, 0, D, HB, r_HB, q_cons)

        KTH = [ARB[:, s * 2048:(s + 1) * 2048].rearrange("p (k t) -> p k t", k=16) for s in range(2)]
        VH = [ARB[:, 4096 + s * 2048:4096 + (s + 1) * 2048].rearrange("p (k t) -> p k t", k=16) for s in range(2)]
        r_KTH = [Res(), Res()]
        r_VH = [Res(), Res()]
        fb = 4096
        bufs = {
            "E": [AR[:, fb + s * 512:fb + (s + 1) * 512] for s in range(2)],
            "L": [AR[:, fb + 1024 + s * 512:fb + 1024 + (s + 1) * 512] for s in range(2)],
            "P": [AR[:, fb + 2048 + s * 520:fb + 2048 + s * 520 + 513] for s in range(2)],
            "rE": [Res(), Res()], "rL": [Res(), Res()], "rP": [Res(), Res()], "rA": [Res(), Res()],
        }
        bb = (fb + 3088) * 2
        bufs["A"] = [ARB[:, bb + s * 512:bb + (s + 1) * 512] for s in range(2)]
        AT_ = [ARB[:, bb + 1024 + s * 512:bb + 1024 + (s + 1) * 512] for s in range(2)]
        r_AT = [Res(), Res()]
        assert bb + 2048 <= 2 * ARN
        arena_phase(r_KTH + r_VH + bufs["rE"] + bufs["rL"] + bufs["rP"] + bufs["rA"] + r_AT)
        for s in range(2):
            P.op("dve", lambda e, s=s: e.memset(bufs["P"][s][:, 0:1], 0.0), writes=[bufs["rP"][s]])

        for h in range(NH):
            hs = h % 2
            ktg = kt_g[h // 8].ap()
            hl = h % 8
            for r in range(2):
                P.dma("sp", lambda e, hs=hs, hl=hl, r=r, ktg=ktg: e.dma_start(
                    out=KTH[hs].rearrange("p (i r) t -> p i r t", r=2)[:, :, r, :],
                    in_=ktg[r * 1024 + hl * 128:r * 1024 + (hl + 1) * 128, :].rearrange("p (i t) -> p i t", t=128)),
                    reads=[r_ktg[h // 8]], writes=[r_KTH[hs]])
                for vh in range(2):
                    vg = v_g[vh].ap()
                    P.dma("sp", lambda e, hs=hs, h=h, r=r, vg=vg, vh=vh: e.dma_start(
                        out=VH[hs].rearrange("p (i r) d -> p i r d", r=2)[:, vh * 4:(vh + 1) * 4, r, :],
                        in_=vg[r * 512:(r + 1) * 512, h * 128:(h + 1) * 128].rearrange("(i t) d -> t i d", t=128)),
                        reads=[r_vg[vh]], writes=[r_VH[hs]])
            for i in range(8):
                nkb = 2 * i + 2
                ob = 4 + (i % 2)
                po, rpo = PS[:, ob, 0:128], r_PS[ob]
                segs = []
                hi = nkb
                while hi > 0:
                    lo = max(0, hi - 4)
                    segs.append((lo, hi))
                    hi = lo
                prev = None
                for si, (lo, hi) in enumerate(segs):
                    n = (hi - lo) * 128
                    ps, rps = next_ps()
                    P.op("pe", lambda e, ps=ps, hs=hs, h=h, i=i, lo=lo, hi=hi, n=n: e.matmul(
                        ps[:, 0:n], lhsT=QB[:, h, i * 128:(i + 1) * 128], rhs=KTH[hs][:, lo:hi, :],
                        start=True, stop=True), reads=[r_QB[h][i // 4], r_KTH[hs]], writes=[rps])
                    s, prev = sb_segment(bufs, n, ps, rps, blb_bc[:, j * 16 + h:j * 16 + h + 1],
                                         mask2 if si == 0 else None, 256, si == 0, prev)
                    pb, rpb = next_pb()
                    for kk in range(hi - lo):
                        P.op("pe", lambda e, pb=pb, s=s, kk=kk: e.transpose(
                            out=pb[:, kk * 128:(kk + 1) * 128], in_=bufs["A"][s][:, kk * 128:(kk + 1) * 128],
                            identity=ident_b), reads=[bufs["rA"][s], r_CB], writes=[rpb])
                    P.op("act", lambda e, pb=pb, s=s, n=n: e.copy(out=AT_[s][:, 0:n], in_=pb[:, 0:n]),
                         reads=[rpb], writes=[r_AT[s]])
                    for kk in range(hi - lo):
                        kb = lo + kk
                        P.op("pe", lambda e, po=po, hs=hs, s=s, kk=kk, kb=kb, si=si, lo=lo, hi=hi: e.matmul(
                            po, lhsT=VH[hs][:, kb, :], rhs=AT_[s][:, kk * 128:(kk + 1) * 128],
                            start=(si == 0 and kk == 0), stop=(lo == 0 and kk == hi - lo - 1)),
                            reads=[r_VH[hs], r_AT[s]], writes=[rpo])
                P.op("act", lambda e, po=po, h=h, i=i: e.copy(out=HB[:, h, i * 128:(i + 1) * 128], in_=po),
                     reads=[rpo], writes=[r_HB[h][i // 4]])

        if do_sample:
            KPG = [ARB[:, s * 2048:(s + 1) * 2048] for s in range(2)]
            VPG = [ARB[:, 4096 + s * 2048:4096 + (s + 1) * 2048] for s in range(2)]
            KT2 = [ARB[:, 8192 + s * 2048:8192 + (s + 1) * 2048] for s in range(2)]
            QP = ARB[:, 12288:14336].rearrange("p (h q) -> p h q", h=NH)
            VSS = ARB[0:8, 14336:16384]
            f2 = 8192
            sbufs = {
                "E": [AR[:, f2 + s * 128:f2 + (s + 1) * 128] for s in range(2)],
                "L": [AR[:, f2 + 256 + s * 128:f2 + 256 + (s + 1) * 128] for s in range(2)],
                "P": [AR[:, f2 + 512 + s * 136:f2 + 512 + s * 136 + 129] for s in range(2)],
                "rE": [Res(), Res()], "rL": [Res(), Res()], "rP": [Res(), Res()], "rA": [Res(), Res()],
            }
            b2 = (f2 + 784) * 2
            sbufs["A"] = [ARB[:, b2 + s * 128:b2 + (s + 1) * 128] for s in range(2)]
            SAT = [ARB[:, b2 + 256 + s * 128:b2 + 256 + (s + 1) * 128] for s in range(2)]
            assert b2 + 512 <= 2 * ARN
            r_KPG = [Res(), Res()]
            r_VPG = [Res(), Res()]
            r_KT2 = [Res(), Res()]
            r_QP = Res()
            r_VSS = Res()
            r_SAT = [Res(), Res()]
            arena_phase(r_KPG + r_VPG + r_KT2 + [r_QP, r_VSS] + sbufs["rE"] + sbufs["rL"] + sbufs["rP"] + sbufs["rA"]
                        + r_SAT)
            for s in range(2):
                P.op("dve", lambda e, s=s: e.memset(sbufs["P"][s][:, 0:1], 0.0), writes=[sbufs["rP"][s]])
            P.op("dve", lambda e: e.memset(QP, 0.0), writes=[r_QP])
            for h in range(NH):
                P.op("dve", lambda e, h=h: e.tensor_copy(out=QP[:, h, h * 8:(h + 1) * 8], in_=QB[:, h, TP:TT]),
                     reads=[r_QB[h][2]], writes=[r_QP])
            P.dma("sp", lambda e: e.dma_start(out=VSS, in_=vs_bd.ap()[:, :]), reads=r_vs_l, writes=[r_VSS])
            po, rpo = PS[:, 4, 0:128], r_PS[4]
            P.op("pe", lambda e: e.matmul(po, lhsT=zeros_b, rhs=zeros_b, start=True, stop=False, skip_group_check=True),
                 reads=[r_CB], writes=[rpo])
            bias_s = blb_hq[:, j:j + 1]
            ps, rps = next_ps()
            for h in range(NH):
                P.op("pe", lambda e, ps=ps, h=h: e.matmul(ps[:, 0:8], lhsT=QP[:, h, :], rhs=KTS[:, h, :],
                                                          start=(h == 0), stop=(h == NH - 1)),
                     reads=[r_QP, r_KTS], writes=[rps])
            s, prev = sb_segment(sbufs, 8, ps, rps, bias_s, masknew, 8, True, None)
            pb, rpb = next_pb()
            P.op("pe", lambda e, pb=pb, s=s: e.transpose(out=pb[0:8, 0:128], in_=sbufs["A"][s][:, 0:8],
                                                         identity=ident_b), reads=[sbufs["rA"][s], r_CB], writes=[rpb])
            P.op("act", lambda e, pb=pb, s=s: e.copy(out=SAT[s][0:8, 0:128], in_=pb[0:8, 0:128]), reads=[rpb],
                 writes=[r_SAT[s]])
            for h in range(NH):
                P.op("pe", lambda e, s=s, h=h: e.matmul(po[:, h * 8:(h + 1) * 8], lhsT=VSS[0:8, h * 128:(h + 1) * 128],
                                                        rhs=SAT[s][0:8, h * 8:(h + 1) * 8], start=False, stop=False,
                                                        skip_group_check=True),
                     reads=[r_VSS, r_SAT[s]], writes=[rpo])
            for pg in range(NPAGES - 1, -1, -1):
                bs = pg % 2
                P.dma("pool", lambda e, bs=bs, pg=pg: e.indirect_dma_start(
                    out=KPG[bs], out_offset=None, in_=cache_k[:, :],
                    in_offset=bass.IndirectOffsetOnAxis(ap=IDX[:, pg:pg + 1], axis=0)),
                    reads=[r_IDX], writes=[r_KPG[bs]])
                P.dma("pool", lambda e, bs=bs, pg=pg: e.indirect_dma_start(
                    out=VPG[bs], out_offset=None, in_=cache_v[:, :],
                    in_offset=bass.IndirectOffsetOnAxis(ap=IDX[:, pg:pg + 1], axis=0)),
                    reads=[r_IDX], writes=[r_VPG[bs]])
                for hq in range(4):
                    pb, rpb = next_pb()
                    for hh in range(4):
                        h = hq * 4 + hh
                        P.op("pe", lambda e, pb=pb, bs=bs, hh=hh, h=h: e.transpose(
                            out=pb[:, hh * 128:(hh + 1) * 128], in_=KPG[bs][:, h * 128:(h + 1) * 128],
                            identity=ident_b), reads=[r_KPG[bs], r_CB], writes=[rpb])
                    if hq % 2 == 0:
                        P.op("act", lambda e, pb=pb, bs=bs, hq=hq: e.copy(
                            out=KT2[bs][:, hq * 512:(hq + 1) * 512], in_=pb[:, :]), reads=[rpb], writes=[r_KT2[bs]])
                    else:
                        P.op("dve", lambda e, pb=pb, bs=bs, hq=hq: e.tensor_copy(
                            out=KT2[bs][:, hq * 512:(hq + 1) * 512], in_=pb[:, :]), reads=[rpb], writes=[r_KT2[bs]])
                ps, rps = next_ps()
                for h in range(NH):
                    P.op("pe", lambda e, ps=ps, h=h, bs=bs: e.matmul(
                        ps[:, 0:128], lhsT=QP[:, h, :], rhs=KT2[bs][:, h * 128:(h + 1) * 128],
                        start=(h == 0), stop=(h == NH - 1)), reads=[r_QP, r_KT2[bs]], writes=[rps])
                s, prev = sb_segment(sbufs, 128, ps, rps, bias_s, None, 0, False, prev)
                pb, rpb = next_pb()
                P.op("pe", lambda e, pb=pb, s=s: e.transpose(out=pb[:, 0:128], in_=sbufs["A"][s][:, 0:128],
                                                             identity=ident_b), reads=[sbufs["rA"][s], r_CB], writes=[rpb])
                P.op("act", lambda e, pb=pb, s=s: e.copy(out=SAT[s][:, 0:128], in_=pb[:, 0:128]), reads=[rpb],
                     writes=[r_SAT[s]])
                for h in range(NH):
                    P.op("pe", lambda e, s=s, h=h, bs=bs, pg=pg: e.matmul(
                        po[:, h * 8:(h + 1) * 8], lhsT=VPG[bs][:, h * 128:(h + 1) * 128],
                        rhs=SAT[s][:, h * 8:(h + 1) * 8], start=False, stop=(pg == 0 and h == NH - 1),
                        skip_group_check=True), reads=[r_VPG[bs], r_SAT[s]], writes=[rpo])
            P.op("act", lambda e: e.copy(out=HB[:, :, TP:TT], in_=po.rearrange("p (h q) -> p h q", h=NH)),
                 reads=[rpo], writes=[r_HB[c][2] for c in range(NCH)])

        linear_fm(b_w_out[j], 0, 16, 0, D, HB, r_HB, add_to_x)

    for j in range(2):
        if stage >= 4 + 2 * j:
            b_mixer(j, stage >= 5 + 2 * j)
            ffn(2 + j)

    YT = AR[:, 0:2048]
    GF = AR[:, 2048:4096]
    JK = AR[:, 4096:6144]
    r_GF = Res()
    r_YT = Res()
    r_JK = Res()
    arena_phase([r_GF, r_YT, r_JK])
    P.dma("sp", lambda e: e.dma_start(out=GF, in_=gfin_d[0:1, :].partition_broadcast(128)), writes=[r_GF])
    ssq = SM[:, 60:61]
    for c in range(9):
        rows = 128 if c < 8 else 8
        g = c // 4 if c < 8 else 2
        for q4 in range(4):
            ps, rps = next_ps()
            for jj in range(4):
                ch = q4 * 4 + jj
                P.op("pe", lambda e, ps=ps, jj=jj, ch=ch, rows=rows, c=c: e.transpose(
                    out=ps[0:rows, jj * 128:(jj + 1) * 128], in_=XT[:, ch, c * 128:c * 128 + rows],
                    identity=ident_f), reads=[r_XT[ch][g], r_CF], writes=[rps])
            P.op("act", lambda e, ps=ps, q4=q4, rows=rows: e.copy(out=YT[0:rows, q4 * 512:(q4 + 1) * 512],
                                                                  in_=ps[0:rows, :]), reads=[rps], writes=[r_YT])
        P.op("act", lambda e, rows=rows: e.activation(out=JK[0:rows, :], in_=YT[0:rows, :], func=AF.Square,
                                                      accum_out=ssq[0:rows, :]), reads=[r_YT], writes=[r_JK])
        P.op("act", lambda e, rows=rows: e.activation(out=ssq[0:rows, :], in_=ssq[0:rows, :], func=AF.Sqrt,
                                                      bias=EPSB[0:rows, :], scale=1.0 / D),
             reads=[r_JK, r_GN], writes=[r_JK])
        P.op("dve", lambda e, rows=rows: e.reciprocal(out=ssq[0:rows, :], in_=ssq[0:rows, :]), reads=[r_JK],
             writes=[r_JK])
        P.op("dve", lambda e, rows=rows: e.scalar_tensor_tensor(out=YT[0:rows, :], in0=YT[0:rows, :],
                                                                scalar=ssq[0:rows, :], in1=GF[0:rows, :],
                                                                op0=ALU.mult, op1=ALU.mult),
             reads=[r_JK, r_YT, r_GF], writes=[r_YT])
        dst = yp[c * 128:(c + 1) * 128, :] if c < 8 else ys[:, :]
        P.dma("sp", lambda e, dst=dst, rows=rows: e.dma_start(out=dst, in_=YT[0:rows, :]), reads=[r_YT],
              is_out=True)

    P.emit()
    st.close()
    return nc


def _consts(parity, blb):
    cf = np.zeros((128, 192), np.float32)
    cf[:, 0:128] = np.eye(128, dtype=np.float32)
    cf[:, 128:130] = np.repeat(blb.reshape(2, 16), 8, axis=1).T
    cf[:, 130:162] = np.broadcast_to(blb.reshape(1, 32), (128, 32))
    cb = np.zeros((128, 1160), np.float32)
    cb[:, 0:128] = np.eye(128, dtype=np.float32)
    cb[:, 128:640] = 1.0
    jj, ii = np.meshgrid(np.arange(128), np.arange(128), indexing="ij")
    cb[:, 640:768] = (jj <= ii).astype(np.float32)
    q, k = np.meshgrid(np.arange(128), np.arange(128), indexing="ij")
    strict = (k < q).astype(np.float32)
    if parity == 0:
        cb[:, 768:896] = strict
        cb[:, 896:1024] = 0.0
    else:
        cb[:, 768:896] = 1.0
        cb[:, 896:1024] = strict
    qq = np.arange(128) % 8
    cb[:, 1024:1032] = (np.arange(8)[None, :] < qq[:, None]).astype(np.float32)
    return cf, cb.astype(ml_dtypes.bfloat16)


_NC_CACHE = {}


def kernel(x_prompt, x_sample, cache_k, cache_v, page_table, norm_mix, norm_ffn, w_ffn_up, w_ffn_down,
           a_w_in, a_norm_v, a_w_spatial, a_b_spatial, a_w_out, kv_norm, w_kv, b_w_q, b_logit_bias, b_w_out,
           norm_final):
    f = lambda a: np.ascontiguousarray(np.asarray(a, dtype=np.float32))
    x_prompt = f(x_prompt)
    x_sample = f(x_sample)
    ck = f(cache_k).reshape(NPOOL * 128, D)
    cv = f(cache_v).reshape(NPOOL * 128, D)
    pt = np.ascontiguousarray(np.asarray(page_table, dtype=np.int32))
    gains = np.ascontiguousarray(np.concatenate([f(norm_mix), f(norm_ffn), f(kv_norm)[None, :]], axis=0))
    blb = f(b_logit_bias)
    shared = {
        "gains": gains, "gv": f(a_norm_v), "gfin": np.ascontiguousarray(f(norm_final)[None, :]),
        "w_ffn_up": f(w_ffn_up), "w_ffn_down": f(w_ffn_down), "a_w_in": f(a_w_in),
        "a_w_spatial": f(a_w_spatial), "a_b_spatial": f(a_b_spatial),
        "a_w_out": f(a_w_out), "w_kv": f(w_kv), "b_w_q": f(b_w_q), "b_w_out": f(b_w_out),
        "cache_k": ck, "cache_v": cv,
    }
    in_maps = []
    for c in range(8):
        s, par = c // 2, c % 2
        xpc = np.ascontiguousarray(x_prompt[s].reshape(16, 128, D)[par::2].reshape(TP, D))
        cf, cb = _consts(par, blb)
        m = dict(shared)
        m.update({"xp": xpc, "xs": np.ascontiguousarray(x_sample[c]), "ptab": np.ascontiguousarray(pt[c:c + 1]),
                  "cst_f": cf, "cst_b": cb})
        in_maps.append(m)
    if "nc" not in _NC_CACHE:
        _NC_CACHE["nc"] = build_program()
    nc = _NC_CACHE["nc"]
    res = run_bass_kernel_spmd(nc, in_maps, core_ids=list(range(8)))
    R = res.results
    y_prompt = np.zeros((4, 16, 128, D), np.float32)
    k_prompt = np.zeros((4, 16, 128, D), np.float32)
    v_prompt = np.zeros((4, 16, 128, D), np.float32)
    for c in range(8):
        s, par = c // 2, c % 2
        y_prompt[s, par::2] = R[c]["yp"].reshape(8, 128, D)
        k_prompt[s, par::2] = R[c]["kp"].reshape(8, 128, D)
        v_prompt[s, par::2] = R[c]["vp"].reshape(8, 128, D)
    y_sample = np.stack([R[c]["ys"] for c in range(8)], axis=0)
    k_sample = np.stack([R[c]["ks"] for c in range(8)], axis=0).reshape(8, 8, NH, 128)
    v_sample = np.stack([R[c]["vs"] for c in range(8)], axis=0).reshape(8, 8, NH, 128)
    sgu = np.stack([R[c]["zs"] for c in range(8)], axis=1)
    return (y_prompt.reshape(4, 2048, D), y_sample, k_prompt.reshape(4, 2048, NH, 128),
            v_prompt.reshape(4, 2048, NH, 128), k_sample, v_sample, np.ascontiguousarray(sgu))
```
